# Optimizing a Trainium2 kernel written in Bass

```python
import math
import jax, jax.numpy as jnp
from jax import lax
import numpy as np

D_MODEL = 1024
BATCH = 2
SEQ = 8192
DEPTH = 1

D_MIX = 2 * D_MODEL
D_SSM = D_MIX // 2
SSM_HEAD_DIM = 64
SSM_HEADS = D_SSM // SSM_HEAD_DIM
SSM_GROUPS = 2
SSM_HEADS_PER_GROUP = SSM_HEADS // SSM_GROUPS
D_STATE = 128
D_CONV = 5
CHUNK = 128
D_XBC = D_SSM + 2 * SSM_GROUPS * D_STATE
D_ATTN = D_MIX - D_SSM
ATTN_HEADS = 8
ATTN_HEAD_DIM = D_ATTN // ATTN_HEADS // 2
ATTN_V_DIM = 2 * ATTN_HEAD_DIM
Q_BLOCK = 128
D_IN_PROJ = D_SSM + D_XBC + 2 * SSM_HEADS + 4 * D_ATTN
ALPHA = (2.0 * DEPTH) ** 0.25
BETA = (8.0 * DEPTH) ** -0.25
LN_EPS = 1e-5
RMS_EPS = 1e-5

kernel_name = "hybrid_ssd_diffattn_deepnorm_encoder"


def layer_norm(x, g, b):
    xf = x.astype(jnp.float32)
    mu = jnp.mean(xf, axis=-1, keepdims=True)
    var = jnp.mean(jnp.square(xf - mu), axis=-1, keepdims=True)
    return ((xf - mu) * lax.rsqrt(var + LN_EPS) * g + b).astype(x.dtype)


def rms_norm(x, g):
    xf = x.astype(jnp.float32)
    ms = jnp.mean(jnp.square(xf), axis=-1, keepdims=True)
    return (xf * lax.rsqrt(ms + RMS_EPS) * g).astype(x.dtype)


def centred_dwconv(u, w, b):
    out = lax.conv_general_dilated(
        u, w[:, None, :].astype(u.dtype), window_strides=(1,),
        padding=[(D_CONV // 2, D_CONV // 2)],
        dimension_numbers=("NWC", "WIO", "NWC"),
        feature_group_count=u.shape[-1])
    return out + b.astype(u.dtype)


def ssd_chunked(x, dt, A, Bm, Cm):
    b, s, g, j, p = x.shape
    n = Bm.shape[-1]
    nc = s // CHUNK
    x = x.reshape(b, nc, CHUNK, g, j, p)
    dt = dt.reshape(b, nc, CHUNK, g, j)
    Bm = Bm.reshape(b, nc, CHUNK, g, n)
    Cm = Cm.reshape(b, nc, CHUNK, g, n)
    a_cs = jnp.cumsum(dt * A, axis=2)
    xdt = x * dt[..., None]
    lower = jnp.tril(jnp.ones((CHUNK, CHUNK), dtype=bool))[None, None, :, :, None, None]
    seg = a_cs[:, :, :, None] - a_cs[:, :, None, :]
    decay = jnp.exp(jnp.where(lower, seg, -jnp.inf))
    cb = jnp.einsum("bclgn,bcsgn->bclsg", Cm, Bm)
    y_diag = jnp.einsum("bclsgj,bcsgjp->bclgjp", cb[..., None] * decay, xdt)
    decay_to_end = jnp.exp(a_cs[:, :, -1:] - a_cs)
    chunk_states = jnp.einsum("bclgn,bclgjp->bcgjpn", Bm, xdt * decay_to_end[..., None])
    chunk_decay = jnp.exp(a_cs[:, :, -1])

    def step(h, inp):
        st, dec = inp
        return dec[..., None, None] * h + st, h

    h0 = jnp.zeros((b, g, j, p, n), x.dtype)
    _, prev = lax.scan(step, h0, (jnp.moveaxis(chunk_states, 1, 0),
                                  jnp.moveaxis(chunk_decay, 1, 0)))
    prev = jnp.moveaxis(prev, 0, 1)
    y_off = jnp.einsum("bclgn,bcgjpn->bclgjp", Cm, prev) * jnp.exp(a_cs)[..., None]
    return (y_diag + y_off).reshape(b, s, g, j, p)


def bidirectional_ssd(xs, dt_f, dt_b, Bm, Cm, A_log_f, A_log_b, dt_bias_f, dt_bias_b, D):
    b, s = xs.shape[:2]
    shp = (SSM_GROUPS, SSM_HEADS_PER_GROUP)
    A_f = -jnp.exp(A_log_f.astype(jnp.float32)).reshape(shp)
    A_b = -jnp.exp(A_log_b.astype(jnp.float32)).reshape(shp)
    dtf = jax.nn.softplus(dt_f.astype(jnp.float32) + dt_bias_f.astype(jnp.float32)).reshape(b, s, *shp)
    dtb = jax.nn.softplus(dt_b.astype(jnp.float32) + dt_bias_b.astype(jnp.float32)).reshape(b, s, *shp)
    xf, Bf, Cf = xs.astype(jnp.float32), Bm.astype(jnp.float32), Cm.astype(jnp.float32)
    flip = lambda t: jnp.flip(t, axis=1)
    y_fwd = ssd_chunked(xf, dtf, A_f, Bf, Cf)
    y_bwd = flip(ssd_chunked(flip(xf), flip(dtb), A_b, flip(Bf), flip(Cf)))
    y = y_fwd + y_bwd + D.astype(jnp.float32).reshape(*shp, 1) * xf
    return y.astype(xs.dtype)


def diff_attention(q, k, v, lam, slopes):
    b, s, h, _, d = q.shape
    nb = s // Q_BLOCK
    scale = ATTN_HEAD_DIM ** -0.5
    k_pos = jnp.arange(s)
    qb = jnp.moveaxis(q.reshape(b, nb, Q_BLOCK, h, 2, d), 1, 0)
    starts = jnp.arange(nb) * Q_BLOCK

    def block(args):
        qi, start = args
        scores = jnp.einsum("bqhrd,bshrd->bhrqs", qi, k).astype(jnp.float32) * scale
        dist = jnp.abs((start + jnp.arange(Q_BLOCK))[:, None] - k_pos[None, :]).astype(jnp.float32)
        scores = scores - slopes[None, :, None, None, None] * dist[None, None, None]
        probs = jax.nn.softmax(scores, axis=-1)
        w = probs[:, :, 0] - lam * probs[:, :, 1]
        return jnp.einsum("bhqs,bshe->bqhe", w.astype(v.dtype), v)

    o = lax.map(block, (qb, starts))
    return jnp.moveaxis(o, 0, 1).reshape(b, s, h, v.shape[-1])


def hybrid_layer(h, layer_idx, w_in, conv_w, conv_b, A_log_f, A_log_b, dt_bias_f, dt_bias_b,
                 D, ssm_norm_g, lq1, lk1, lq2, lk2, subln_g, w_out, ln_g, ln_b):
    b, s, _ = h.shape
    proj = jnp.einsum("bsd,de->bse", h, w_in.astype(h.dtype))
    offs = np.cumsum([D_SSM, D_XBC, SSM_HEADS, SSM_HEADS, D_ATTN, D_ATTN, D_ATTN, D_ATTN])[:-1].tolist()
    z, xbc, dt_f, dt_b, q, k, v, g = jnp.split(proj, offs, axis=-1)

    xbc = jax.nn.silu(centred_dwconv(xbc, conv_w, conv_b))
    xs, Bm, Cm = jnp.split(xbc, [D_SSM, D_SSM + SSM_GROUPS * D_STATE], axis=-1)
    xs = xs.reshape(b, s, SSM_GROUPS, SSM_HEADS_PER_GROUP, SSM_HEAD_DIM)
    Bm = Bm.reshape(b, s, SSM_GROUPS, D_STATE)
    Cm = Cm.reshape(b, s, SSM_GROUPS, D_STATE)
    y = bidirectional_ssd(xs, dt_f, dt_b, Bm, Cm, A_log_f, A_log_b, dt_bias_f, dt_bias_b, D)
    y = y.reshape(b, s, SSM_GROUPS, D_SSM // SSM_GROUPS)
    zg = jax.nn.silu(z).reshape(b, s, SSM_GROUPS, D_SSM // SSM_GROUPS)
    y_ssm = rms_norm(y * zg, ssm_norm_g.reshape(SSM_GROUPS, -1)).reshape(b, s, D_SSM)

    q = q.reshape(b, s, ATTN_HEADS, 2, ATTN_HEAD_DIM)
    k = k.reshape(b, s, ATTN_HEADS, 2, ATTN_HEAD_DIM)
    v = v.reshape(b, s, ATTN_HEADS, ATTN_V_DIM)
    lam_init = 0.8 - 0.6 * math.exp(-0.3 * layer_idx)
    lam = (jnp.exp(jnp.sum(lq1.astype(jnp.float32) * lk1.astype(jnp.float32)))
           - jnp.exp(jnp.sum(lq2.astype(jnp.float32) * lk2.astype(jnp.float32))) + lam_init)
    slopes = jnp.exp2(-8.0 * (jnp.arange(ATTN_HEADS, dtype=jnp.float32) + 1.0) / ATTN_HEADS)
    o = diff_attention(q, k, v, lam, slopes)
    o = rms_norm(o, subln_g) * (1.0 - lam_init)
    y_attn = o.reshape(b, s, D_ATTN) * jax.nn.silu(g)

    mix = jnp.einsum("bse,ed->bsd", jnp.concatenate([y_ssm, y_attn], axis=-1), w_out.astype(h.dtype))
    return layer_norm(ALPHA * h + mix, ln_g, ln_b)


def setup_inputs(seed: int = 0) -> dict:
    key = jax.random.key(seed)
    ks = jax.random.split(key, 24)
    f32 = jnp.float32
    x = jax.random.normal(ks[0], (BATCH, SEQ, D_MODEL), f32)
    ln_emb_g = 1.0 + 0.02 * jax.random.normal(ks[1], (D_MODEL,), f32)
    ln_emb_b = 0.02 * jax.random.normal(ks[2], (D_MODEL,), f32)
    x_start = D_SSM
    v_start = D_SSM + D_XBC + 2 * SSM_HEADS + 2 * D_ATTN
    col_scale = (jnp.ones((D_IN_PROJ,), f32)
                 .at[x_start:x_start + D_SSM].set(BETA)
                 .at[v_start:v_start + D_ATTN].set(BETA))
    w_in = jax.random.normal(ks[3], (DEPTH, D_MODEL, D_IN_PROJ), f32) * (D_MODEL ** -0.5) * col_scale
    conv_w = jax.random.normal(ks[4], (DEPTH, D_CONV, D_XBC), f32) * (D_CONV ** -0.5)
    conv_b = 0.01 * jax.random.normal(ks[5], (DEPTH, D_XBC), f32)
    A_log_fwd = jnp.log(jax.random.uniform(ks[6], (DEPTH, SSM_HEADS), f32, 1.0, 16.0))
    A_log_bwd = jnp.log(jax.random.uniform(ks[7], (DEPTH, SSM_HEADS), f32, 1.0, 16.0))
    dt_f0 = jnp.exp(jax.random.uniform(ks[8], (DEPTH, SSM_HEADS), f32, math.log(1e-3), math.log(1e-1)))
    dt_b0 = jnp.exp(jax.random.uniform(ks[9], (DEPTH, SSM_HEADS), f32, math.log(1e-3), math.log(1e-1)))
    dt_bias_fwd = dt_f0 + jnp.log(-jnp.expm1(-dt_f0))
    dt_bias_bwd = dt_b0 + jnp.log(-jnp.expm1(-dt_b0))
    D_skip = 1.0 + 0.1 * jax.random.normal(ks[10], (DEPTH, SSM_HEADS), f32)
    ssm_norm_g = 1.0 + 0.02 * jax.random.normal(ks[11], (DEPTH, D_SSM), f32)
    lambda_q1 = 0.1 * jax.random.normal(ks[12], (DEPTH, ATTN_HEAD_DIM), f32)
    lambda_k1 = 0.1 * jax.random.normal(ks[13], (DEPTH, ATTN_HEAD_DIM), f32)
    lambda_q2 = 0.1 * jax.random.normal(ks[14], (DEPTH, ATTN_HEAD_DIM), f32)
    lambda_k2 = 0.1 * jax.random.normal(ks[15], (DEPTH, ATTN_HEAD_DIM), f32)
    subln_g = 1.0 + 0.02 * jax.random.normal(ks[16], (DEPTH, ATTN_V_DIM), f32)
    w_out = jax.random.normal(ks[17], (DEPTH, D_MIX, D_MODEL), f32) * (D_MIX ** -0.5) * BETA
    ln_g = 1.0 + 0.02 * jax.random.normal(ks[18], (DEPTH, D_MODEL), f32)
    ln_b = 0.02 * jax.random.normal(ks[19], (DEPTH, D_MODEL), f32)
    return {"x": x, "ln_emb_g": ln_emb_g, "ln_emb_b": ln_emb_b, "w_in": w_in,
            "conv_w": conv_w, "conv_b": conv_b, "A_log_fwd": A_log_fwd, "A_log_bwd": A_log_bwd,
            "dt_bias_fwd": dt_bias_fwd, "dt_bias_bwd": dt_bias_bwd, "D_skip": D_skip,
            "ssm_norm_g": ssm_norm_g, "lambda_q1": lambda_q1, "lambda_k1": lambda_k1,
            "lambda_q2": lambda_q2, "lambda_k2": lambda_k2, "subln_g": subln_g,
            "w_out": w_out, "ln_g": ln_g, "ln_b": ln_b}


def reference(x, ln_emb_g, ln_emb_b, w_in, conv_w, conv_b, A_log_fwd, A_log_bwd,
              dt_bias_fwd, dt_bias_bwd, D_skip, ssm_norm_g, lambda_q1, lambda_k1,
              lambda_q2, lambda_k2, subln_g, w_out, ln_g, ln_b):
    h = layer_norm(x, ln_emb_g, ln_emb_b)
    for l in range(DEPTH):
        h = hybrid_layer(h, l, w_in[l], conv_w[l], conv_b[l], A_log_fwd[l], A_log_bwd[l],
                         dt_bias_fwd[l], dt_bias_bwd[l], D_skip[l], ssm_norm_g[l],
                         lambda_q1[l], lambda_k1[l], lambda_q2[l], lambda_k2[l],
                         subln_g[l], w_out[l], ln_g[l], ln_b[l])
    return h
```

```python
import math
from contextlib import ExitStack

import numpy as np
import ml_dtypes

import concourse.bass as bass
import concourse.mybir as mybir
from concourse.bass_utils import run_bass_kernel_spmd

F32 = mybir.dt.float32
BF16 = mybir.dt.bfloat16
ALU = mybir.AluOpType
AF = mybir.ActivationFunctionType
AX = mybir.AxisListType

S = 8192
D = 1024
NT = S // 128
LN_EPS = 1e-5
RMS_EPS = 1e-5
ALPHA = 2.0 ** 0.25
LAM_INIT = 0.8 - 0.6 * math.exp(0.0)
NFM = 1024
NTM = 776


class Eng:
    def __init__(self, nc, raw, name, has_sem=True):
        self.raw = raw
        self.name = name
        self.sem = nc.alloc_semaphore("prog_" + name) if has_sem else None
        self.count = 0
        self.seen = {}

    def wait(self, ev):
        if ev is None:
            return
        sem, val = ev
        if self.seen.get(sem, 0) >= val:
            return
        self.seen[sem] = val
        self.raw.wait_ge(sem, val)


class DSem:
    def __init__(self, handle):
        self.handle = handle
        self.count = 0


class Buf:
    def __init__(self, name="", stream=False):
        self.name = name
        self.w = None
        self.r = {}
        self.dsem = None
        self.stream = stream


class FW:
    def __init__(self, nc):
        self.nc = nc
        self.pe = Eng(nc, nc.tensor, "pe")
        self.act = Eng(nc, nc.scalar, "act")
        self.dve = Eng(nc, nc.vector, "dve")
        self.pool = Eng(nc, nc.gpsimd, "pool")
        self.sp = Eng(nc, nc.sync, "sp", has_sem=False)
        self.engs = [self.pe, self.act, self.dve, self.pool, self.sp]
        self.free_dsems = []
        self.all_dsems = []
        self.phase_bufs = []

    def alloc_dsem(self):
        if self.free_dsems:
            return self.free_dsems.pop()
        d = DSem(self.nc.alloc_semaphore("dma%d" % len(self.all_dsems)))
        self.all_dsems.append(d)
        return d

    def _pre(self, eng, reads, writes):
        for b in reads:
            eng.wait(b.w)
        for b in writes:
            if not b.stream:
                eng.wait(b.w)
            for sem, val in list(b.r.items()):
                eng.wait((sem, val))

    def op(self, eng, fn, reads=(), writes=()):
        self._pre(eng, reads, writes)
        ins = fn(eng.raw)
        eng.count += 1
        ins.then_inc(eng.sem, 1)
        ev = (eng.sem, eng.count)
        for b in reads:
            b.r[eng.sem] = eng.count
        for b in writes:
            b.w = ev
            b.r = {}
        return ev

    def dma(self, q, out_ap, in_ap, reads=(), writes=(), **kw):
        assert len(writes) == 1
        wb = writes[0]
        self._pre(q, reads, writes)
        if wb.dsem is None:
            wb.dsem = self.alloc_dsem()
            self.phase_bufs.append(wb)
        ins = q.raw.dma_start(out=out_ap, in_=in_ap, **kw)
        wb.dsem.count += 16
        ins.then_inc(wb.dsem.handle, 16)
        ev = (wb.dsem.handle, wb.dsem.count)
        for b in reads:
            b.r[wb.dsem.handle] = wb.dsem.count
        wb.w = ev
        wb.r = {}
        return ev

    def barrier(self, release=True):
        for e in self.engs:
            for f in self.engs:
                if f.sem is not None and f.count > 0:
                    e.wait((f.sem, f.count))
            for d in self.all_dsems:
                if d.count > 0:
                    e.wait((d.handle, d.count))
        if release:
            for b in self.phase_bufs:
                if b.dsem is not None:
                    self.free_dsems.append(b.dsem)
                    b.dsem = None
                b.w = None
                b.r = {}
            self.phase_bufs = []


def core_heads(r):
    return (r, 7 - r)


def build_program(phases=("p1", "ssd", "attn", "p5"), debug=False, attn_steps=None, p5_stage=9):
    nc = bass.Bass("TRN2", target_bir_lowering=False)
    fw = FW(nc)
    pe, act, dve, pool, sp = fw.pe, fw.act, fw.dve, fw.pool, fw.sp

    def din(name, shape, dt=F32):
        return nc.dram_tensor(name, list(shape), dt, kind="ExternalInput").ap()

    def dscr(name, shape, dt):
        kind = "ExternalOutput" if (debug and name in debug) else "Internal"
        t = nc.dram_tensor(name, list(shape), dt, kind=kind)
        return t

    xb = din("xb", [S, D])
    wfm_d = din("wfm", [D, NFM])
    wtm_d = din("wtm", [D, NTM])
    lng_d = din("lneg", [128, 8])
    lnb_d = din("lneb", [128, 8])
    ident_d = din("ident", [128, 128], BF16)
    blk2_d = din("blk2", [128, 2])

    QT = dscr("QT", [256, S], BF16)
    KT = dscr("KT", [256, S], BF16)
    XT = dscr("XT", [256, S], F32)
    BT = dscr("BT", [128, S], F32)
    CT = dscr("CT", [128, S], F32)
    TM = dscr("TM", [S, 768], BF16)
    DT = dscr("DT", [S, 8], F32)
    NRM = dscr("NRM", [2, 4], F32)
    bQT, bKT, bXT, bBT, bCT, bTM, bDT, bNRM = [Buf(n, stream=True) for n in
                                             ("QT", "KT", "XT", "BT", "CT", "TM", "DT", "NRM")]

    out_d = nc.dram_tensor("out", [2048, D], F32, kind="ExternalOutput").ap()

    if "p1" in phases:
        with ExitStack() as es:
            def sb(name, shape, dt):
                return es.enter_context(nc.sbuf_tensor(name, list(shape), dt))

            def ps(name, shape, dt=F32):
                return es.enter_context(nc.psum_tensor(name, list(shape), dt))

            wstage = sb("wstage", [128, 8, NFM], F32)
            wfm = sb("wfm_sb", [128, 8, NFM], BF16)
            wtm = sb("wtm_sb", [128, 8, NTM], BF16)
            lng = sb("lng", [128, 8], F32)
            lnb = sb("lnb", [128, 8], F32)
            ident = sb("ident_sb", [128, 128], BF16)
            blk2 = sb("blk2_sb", [128, 2], F32)
            nstat = sb("nstat", [2, 4], F32)
            xt = [sb("xt%d" % i, [128, 4, D], F32) for i in range(2)]
            hn = [sb("hn%d" % i, [128, D], BF16) for i in range(2)]
            hT = [sb("hT%d" % i, [128, 8, 512], BF16) for i in range(2)]
            st = [sb("st%d" % i, [128, 12], F32) for i in range(2)]
            mv = [sb("mv%d" % i, [128, 2], F32) for i in range(2)]
            rstd = [sb("rstd%d" % i, [128, 1], F32) for i in range(2)]
            fmo_b = [sb("fmob%d" % i, [128, 512], BF16) for i in range(2)]
            fmo_f = [sb("fmof%d" % i, [128, 512], F32) for i in range(2)]
            sq = [sb("sq%d" % i, [128, 512], F32) for i in range(2)]
            nmx = [sb("nmx%d" % i, [2, 1], F32) for i in range(2)]
            tmo = [sb("tmo%d" % i, [128, 768], BF16) for i in range(2)]
            dto = [sb("dto%d" % i, [128, 8], F32) for i in range(2)]

            psT = [ps("psT%d" % i, [128, 1024], BF16) for i in range(2)]
            psF = [ps("psF%d" % i, [128, 512]) for i in range(2)]
            psA = [ps("psA%d" % i, [128, 512]) for i in range(1)]
            psB = [ps("psB%d" % i, [128, 512]) for i in range(1)]
            psN = [ps("psN%d" % i, [128, 512]) for i in range(2)]

            B = lambda n: Buf(n)
            b_wstage, b_wfm, b_wtm, b_lng, b_lnb, b_ident, b_blk2, b_nstat = [B(n) for n in
                ("wstage", "wfm", "wtm", "lng", "lnb", "ident", "blk2", "nstat")]
            b_xt = [B("xt") for _ in range(2)]
            b_hn = [B("hn") for _ in range(2)]
            b_hT = [B("hT") for _ in range(2)]
            b_st = [B("st") for _ in range(2)]
            b_mv = [B("mv") for _ in range(2)]
            b_rstd = [B("rstd") for _ in range(2)]
            b_fmob = [B("fmob") for _ in range(2)]
            b_fmof = [B("fmof") for _ in range(2)]
            b_sq = [B("sq") for _ in range(2)]
            b_nmx = [B("nmx") for _ in range(2)]
            b_tmo = [B("tmo") for _ in range(2)]
            b_dto = [B("dto") for _ in range(2)]
            b_psT = [B("psT") for _ in range(2)]
            b_psF = [B("psF") for _ in range(2)]
            b_psA = [B("psA")]
            b_psB = [B("psB")]
            b_psN = [B("psN") for _ in range(2)]

            fw.dma(sp, lng[:, :], lng_d, writes=[b_lng])
            fw.dma(sp, lnb[:, :], lnb_d, writes=[b_lnb])
            fw.dma(sp, ident[:, :], ident_d, writes=[b_ident])
            fw.dma(sp, blk2[:, :], blk2_d, writes=[b_blk2])
            fw.op(dve, lambda e: e.memset(nstat[:, :], 0.0), writes=[b_nstat])
            fw.dma(sp, wstage[:, :, :], wfm_d.rearrange("(k p) n -> p k n", p=128), writes=[b_wstage])
            fw.op(pool, lambda e: e.tensor_copy(wfm[:, :, :], wstage[:, :, :]),
                  reads=[b_wstage], writes=[b_wfm])
            fw.dma(sp, wstage[:, :, 0:NTM], wtm_d.rearrange("(k p) n -> p k n", p=128),
                   writes=[b_wstage])
            fw.op(pool, lambda e: e.tensor_copy(wtm[:, :, :], wstage[:, :, 0:NTM]),
                  reads=[b_wstage], writes=[b_wtm])

            xview = xb.rearrange("(t a p) d -> t p a d", a=4, p=128)
            fm_cnt = 0
            sub_cnt = 0
            for T in range(16):
                xs = T % 2
                fw.dma(sp, xt[xs][:, :, :], xview[T], writes=[b_xt[xs]])
                for a in range(4):
                    k = sub_cnt % 2
                    sub_cnt += 1
                    xa = xt[xs][:, a, :]
                    fw.op(dve, lambda e: (e.bn_stats(st[k][:, 0:6], xa[:, 0:512]),
                                          e.bn_stats(st[k][:, 6:12], xa[:, 512:1024]))[1],
                          reads=[b_xt[xs]], writes=[b_st[k]])
                    fw.op(dve, lambda e: e.bn_aggr(mv[k][:, :], st[k][:, :]),
                          reads=[b_st[k]], writes=[b_mv[k]])
                    fw.op(act, lambda e: e.activation(rstd[k][:, :], mv[k][:, 1:2], AF.Sqrt, bias=LN_EPS, scale=1.0),
                          reads=[b_mv[k]], writes=[b_rstd[k]])
                    fw.op(dve, lambda e: e.reciprocal(rstd[k][:, :], rstd[k][:, :]),
                          reads=[b_rstd[k]], writes=[b_rstd[k]])
                    fw.op(dve, lambda e: e.tensor_scalar(hn[k][:, :], xa, mv[k][:, 0:1], rstd[k][:, :],
                                                         ALU.subtract, ALU.mult),
                          reads=[b_xt[xs], b_mv[k], b_rstd[k]], writes=[b_hn[k]])

                    def tr(e):
                        ins = None
                        for dk in range(8):
                            ins = e.transpose(psT[k][:, dk * 128:(dk + 1) * 128],
                                              hn[k][:, dk * 128:(dk + 1) * 128], ident[:, :])
                        return ins
                    fw.op(pe, tr, reads=[b_hn[k], b_ident], writes=[b_psT[k]])

                    def ev_t(e):
                        ins = None
                        for dk in range(8):
                            ins = e.tensor_scalar(hT[xs][:, dk, a * 128:(a + 1) * 128],
                                                  psT[k][:, dk * 128:(dk + 1) * 128],
                                                  lng[:, dk:dk + 1], lnb[:, dk:dk + 1],
                                                  ALU.mult, ALU.add)
                        return ins
                    fw.op(dve, ev_t, reads=[b_psT[k], b_lng, b_lnb], writes=[b_hT[xs]])

                    def mm_a(e):
                        ins = None
                        for dk in range(8):
                            ins = e.matmul(psA[0][:, 0:512], hT[xs][:, dk, a * 128:(a + 1) * 128],
                                           wtm[:, dk, 0:512], start=(dk == 0), stop=(dk == 7))
                        return ins
                    fw.op(pe, mm_a, reads=[b_hT[xs], b_wtm], writes=[b_psA[0]])

                    def mm_b(e):
                        ins = None
                        for dk in range(8):
                            ins = e.matmul(psB[0][:, 0:264], hT[xs][:, dk, a * 128:(a + 1) * 128],
                                           wtm[:, dk, 512:776], start=(dk == 0), stop=(dk == 7))
                        return ins
                    fw.op(pe, mm_b, reads=[b_hT[xs], b_wtm], writes=[b_psB[0]])
                    fw.op(act, lambda e: e.activation(tmo[k][:, 0:256], psA[0][:, 0:256], AF.Copy),
                          reads=[b_psA[0]], writes=[b_tmo[k]])
                    fw.op(act, lambda e: e.activation(tmo[k][:, 256:512], psA[0][:, 256:512], AF.Silu),
                          reads=[b_psA[0]], writes=[b_tmo[k]])
                    fw.op(act, lambda e: e.activation(tmo[k][:, 512:768], psB[0][:, 0:256], AF.Silu),
                          reads=[b_psB[0]], writes=[b_tmo[k]])
                    fw.op(dve, lambda e: e.tensor_copy(dto[k][:, :], psB[0][:, 256:264]),
                          reads=[b_psB[0]], writes=[b_dto[k]])
                    tok0 = T * 512 + a * 128
                    fw.dma(pool, TM[tok0:tok0 + 128, :], tmo[k][:, :], reads=[b_tmo[k]], writes=[bTM])
                    fw.dma(pool, DT[tok0:tok0 + 128, :], dto[k][:, :], reads=[b_dto[k]], writes=[bDT])

                for c in range(8):
                    f = fm_cnt % 2
                    fm_cnt += 1

                    def mm_f(e):
                        ins = None
                        for dk in range(8):
                            ins = e.matmul(psF[f][:, :], wfm[:, dk, c * 128:(c + 1) * 128],
                                           hT[xs][:, dk, :], start=(dk == 0), stop=(dk == 7))
                        return ins
                    fw.op(pe, mm_f, reads=[b_hT[xs], b_wfm], writes=[b_psF[f]])
                    tsl = slice(T * 512, (T + 1) * 512)
                    if c < 4:
                        scl = 0.125 if c < 2 else 1.0
                        fw.op(act, lambda e: e.activation(fmo_b[f][:, :], psF[f][:, :], AF.Copy, scale=scl),
                              reads=[b_psF[f]], writes=[b_fmob[f]])
                        fw.op(act, lambda e: e.activation(sq[f][:, :], psF[f][:, :], AF.Square, scale=scl),
                              reads=[b_psF[f]], writes=[b_sq[f]])
                        dst = QT if c < 2 else KT
                        bd = bQT if c < 2 else bKT
                        hl = c % 2
                        fw.dma(pool, dst[hl * 128:(hl + 1) * 128, tsl], fmo_b[f][:, :],
                               reads=[b_fmob[f]], writes=[bd])
                        fw.op(pe, lambda e: e.matmul(psN[f][0:2, :], blk2[:, :], sq[f][:, :],
                                                     start=True, stop=True),
                              reads=[b_sq[f], b_blk2], writes=[b_psN[f]])
                        fw.op(dve, lambda e: e.reduce_max(nmx[f][:, :], psN[f][0:2, :], AX.X),
                              reads=[b_psN[f]], writes=[b_nmx[f]])
                        fw.op(dve, lambda e: e.tensor_max(nstat[:, c:c + 1], nstat[:, c:c + 1], nmx[f][:, :]),
                              reads=[b_nmx[f], b_nstat], writes=[b_nstat])
                    else:
                        fw.op(dve, lambda e: e.tensor_copy(fmo_f[f][:, :], psF[f][:, :]),
                              reads=[b_psF[f]], writes=[b_fmof[f]])
                        if c < 6:
                            dst_ap = XT[(c - 4) * 128:(c - 3) * 128, tsl]
                            bd = bXT
                        elif c == 6:
                            dst_ap = BT[:, tsl]
                            bd = bBT
                        else:
                            dst_ap = CT[:, tsl]
                            bd = bCT
                        fw.dma(pool, dst_ap, fmo_f[f][:, :], reads=[b_fmof[f]], writes=[bd])
            fw.dma(pool, NRM[:, :], nstat[:, :], reads=[b_nstat], writes=[bNRM])
            fw.barrier(release=False)


    YA = dscr("YA", [S, 256], BF16)
    bYA = Buf("YA", stream=True)
    if "attn" in phases:
        posr_d = din("posr", [2, 2, 2, 512], BF16)
        tab_d = din("tab", [128, 2, 126])
        E_d = din("Etab", [128, 2, 128])
        lamv_d = din("lamv", [4, 64])
        subg_d = din("subg", [1, 128])
        fw.barrier(release=True)
        with ExitStack() as es:
            def sb(name, shape, dt):
                return es.enter_context(nc.sbuf_tensor(name, list(shape), dt))

            def ps(name, shape, dt=F32):
                return es.enter_context(nc.psum_tensor(name, list(shape), dt))

            Kaug = [sb("Kaug%d" % hm, [66, S], BF16) for hm in range(4)]
            Vp = sb("Vp", [128, 64, 2, 130], BF16)
            Qa = [[[sb("Qa%d_%d_%d" % (sl, hm, v), [66, 512], BF16) for v in range(2)]
                   for hm in range(4)] for sl in range(2)]
            gq = [sb("gq%d" % i, [128, 4, 256], BF16) for i in range(3)]
            Pt = [sb("Pt%d" % i, [128, 512], BF16) for i in range(4)]
            Ptmp = [sb("Ptmp%d" % i, [128, 128], F32) for i in range(2)]
            btab = sb("btab", [128, 4, 126], F32)
            tab = sb("tab_sb", [128, 2, 126], F32)
            Etab = sb("Etab_sb", [128, 2, 128], F32)
            nrm = sb("nrm_sb", [128, 8], F32)
            negC = sb("negC", [128, 4], F32)
            lamv = sb("lamv_sb", [128, 4, 64], F32)
            lamp = sb("lamp", [128, 2, 64], F32)
            lams = sb("lams", [128, 2], F32)
            neglam = sb("neglam", [128, 1], F32)
            gsc = sb("gsc", [128, 128], F32)
            rr = [sb("rr%d" % i, [128, 4], F32) for i in range(2)]
            t1 = [sb("t1_%d" % i, [128, 128], F32) for i in range(2)]
            ot = [sb("ot%d" % i, [128, 128], F32) for i in range(2)]
            junk = [sb("junk%d" % i, [128, 128], F32) for i in range(2)]
            yo = [sb("yo%d" % i, [128, 4, 128], BF16) for i in range(2)]
            zl = sb("zl", [128, 128], BF16)
            zr = sb("zr", [128, 512], BF16)

            Sb = [ps("Sb%d" % i, [128, 512]) for i in range(2)]
            Ob = [[ps("Ob%d_%d" % (s_, i), [128, 512]) for i in range(3)] for s_ in range(2)]

            b_K = [Buf() for _ in range(4)]
            b_Kones = [Buf() for _ in range(4)]
            b_V = [Buf() for _ in range(16)]
            b_Vones = Buf()
            b_Qa = [[[Buf() for v in range(2)] for hm in range(4)] for sl in range(2)]
            b_Qpos = [[[Buf() for v in range(2)] for hm in range(4)] for sl in range(2)]
            b_gq = [Buf() for _ in range(3)]
            b_Pt = [Buf() for _ in range(4)]
            b_Ptmp = [Buf() for _ in range(2)]
            b_btab, b_tab, b_E, b_nrm, b_negC, b_lamv, b_lamp, b_lams, b_neglam, b_gsc = [Buf() for _ in range(10)]
            b_rr = [Buf() for _ in range(2)]
            b_t1 = [Buf() for _ in range(2)]
            b_ot = [Buf() for _ in range(2)]
            b_junk = [Buf() for _ in range(2)]
            b_yo = [Buf() for _ in range(2)]
            b_Sb = [Buf() for _ in range(2)]
            b_Ob = [[Buf() for _ in range(3)] for _ in range(2)]

            b_zz = Buf()
            fw.op(dve, lambda e: e.memset(zl[:, :], 0.0), writes=[b_zz])
            fw.op(dve, lambda e: e.memset(zr[:, :], 0.0), writes=[b_zz])
            fw.dma(sp, tab[:, :, :], tab_d, writes=[b_tab])
            fw.dma(sp, Etab[:, :, :], E_d, writes=[b_E])
            fw.dma(sp, nrm[:, :], NRM[:, :].rearrange("a b -> (a b)").partition_broadcast(128),
                   reads=[bNRM], writes=[b_nrm])
            fw.dma(sp, lamv[:, :, :].rearrange("p a b -> p (a b)"),
                   lamv_d.rearrange("a b -> (a b)").partition_broadcast(128), writes=[b_lamv])
            fw.dma(sp, gsc[:, :], subg_d.rearrange("a b -> (a b)").partition_broadcast(128), writes=[b_gsc])
            for hm in range(4):
                fw.dma(sp, Kaug[hm][0:64, :], KT[hm * 64:(hm + 1) * 64, :], reads=[bKT], writes=[b_K[hm]])
                fw.op(dve, lambda e: e.memset(Kaug[hm][64:66, :], 1.0), writes=[b_Kones[hm]])
            tmv = TM[:, :].rearrange("(j p) c -> p j c", p=128)
            for jg in range(8):
                for hl in range(2):
                    fw.dma(sp, Vp[:, jg * 8:(jg + 1) * 8, hl, 0:128],
                           tmv[:, jg * 8:(jg + 1) * 8, hl * 128:(hl + 1) * 128],
                           reads=[bTM], writes=[b_V[jg * 2 + hl]])
            fw.op(dve, lambda e: e.memset(Vp[:, :, :, 128:129], 1.0), writes=[b_Vones])
            for sl in range(2):
                for hm in range(4):
                    for v in range(2):
                        fw.dma(sp, Qa[sl][hm][v][64:66, :], posr_d[hm // 2, v], writes=[b_Qpos[sl][hm][v]])
            for m in range(2):
                fw.op(dve, lambda e: e.tensor_tensor(negC[:, m * 2:m * 2 + 2], nrm[:, m * 4:m * 4 + 2],
                                                     nrm[:, m * 4 + 2:m * 4 + 4], ALU.mult),
                      reads=[b_nrm], writes=[b_negC])
            fw.op(act, lambda e: e.activation(negC[:, :], negC[:, :], AF.Sqrt), reads=[b_negC], writes=[b_negC])
            fw.op(dve, lambda e: e.tensor_scalar_mul(negC[:, :], negC[:, :], -1.0), reads=[b_negC], writes=[b_negC])
            for hm in range(4):
                hl, m = hm // 2, hm % 2
                fw.op(dve, lambda e: e.tensor_scalar_add(btab[:, hm, :], tab[:, hl, :], negC[:, m * 2 + hl:m * 2 + hl + 1]),
                      reads=[b_tab, b_negC], writes=[b_btab])
            fw.op(dve, lambda e: e.tensor_tensor(lamp[:, 0, :], lamv[:, 0, :], lamv[:, 1, :], ALU.mult),
                  reads=[b_lamv], writes=[b_lamp])
            fw.op(dve, lambda e: e.tensor_tensor(lamp[:, 1, :], lamv[:, 2, :], lamv[:, 3, :], ALU.mult),
                  reads=[b_lamv], writes=[b_lamp])
            fw.op(dve, lambda e: e.reduce_sum(lams[:, :], lamp[:, :, :], AX.X), reads=[b_lamp], writes=[b_lams])
            fw.op(act, lambda e: e.activation(lams[:, :], lams[:, :], AF.Exp), reads=[b_lams], writes=[b_lams])
            fw.op(dve, lambda e: e.scalar_tensor_tensor(neglam[:, :], lams[:, 1:2], -LAM_INIT, lams[:, 0:1],
                                                        ALU.add, ALU.subtract),
                  reads=[b_lams], writes=[b_neglam])
            fw.op(dve, lambda e: e.tensor_scalar_mul(gsc[:, :], gsc[:, :], 1.0 - LAM_INIT), reads=[b_gsc], writes=[b_gsc])

            tmg = TM[:, :].rearrange("(i u p) c -> i p u c", u=4, p=128)
            yav = YA[:, :].rearrange("(i u p) c -> i p u c", u=4, p=128)

            def load_q(i):
                sl = i % 2
                for hm in range(4):
                    for v in range(2):
                        fw.dma(sp, Qa[sl][hm][v][0:64, :], QT[hm * 64:(hm + 1) * 64, i * 512:(i + 1) * 512],
                               reads=[bQT], writes=[b_Qa[sl][hm][v]])
                fw.dma(sp, gq[i % 3][:, :, :], tmg[i][:, :, 256:512], reads=[bTM], writes=[b_gq[i % 3]])

            steps = [(i, hl, j, m) for i in range(16) for hl in range(2) for j in range(64) for m in range(2)]
            fin_cnt = [0]

            def qk_exp(k):
                i, hl, j, m = steps[k]
                hm = hl * 2 + m
                sl = i % 2
                sbk = k % 2
                pk = k % 4
                if hl == 0 and j == 0 and m == 0:
                    if i == 0:
                        load_q(0)
                    if i + 1 < 16:
                        load_q(i + 1)
                kreads = [b_K[hm], b_Kones[hm]]
                ksl = slice(j * 128, (j + 1) * 128)
                if j < 4 * i or j > 4 * i + 3:
                    v = 0 if j < 4 * i else 1
                    idx = (4 * i - j + 2) if v == 0 else (63 + j - 4 * i - 1)
                    fw.op(pe, lambda e: e.matmul(Sb[sbk][:, :], Kaug[hm][0:66, ksl], Qa[sl][hm][v][0:66, :],
                                                 start=True, stop=True),
                          reads=kreads + [b_Qa[sl][hm][v], b_Qpos[sl][hm][v]], writes=[b_Sb[sbk]])
                    fw.op(act, lambda e: e.activation(Pt[pk][:, :], Sb[sbk][:, :], AF.Exp,
                                                      bias=btab[:, hm, idx:idx + 1], scale=1.0),
                          reads=[b_Sb[sbk], b_btab], writes=[b_Pt[pk]])
                else:
                    def mm(e):
                        ins = None
                        for u in range(4):
                            usl = slice(u * 128, (u + 1) * 128)
                            jj = 4 * i + u
                            if j == jj:
                                ins = e.matmul(Sb[sbk][:, usl], Kaug[hm][0:64, ksl], Qa[sl][hm][0][0:64, usl],
                                               start=True, stop=True)
                            else:
                                v = 0 if j < jj else 1
                                ins = e.matmul(Sb[sbk][:, usl], Kaug[hm][0:66, ksl], Qa[sl][hm][v][0:66, usl],
                                               start=True, stop=True)
                        return ins
                    fw.op(pe, mm, reads=kreads + [b_Qa[sl][hm][0], b_Qpos[sl][hm][0], b_Qa[sl][hm][1], b_Qpos[sl][hm][1]],
                          writes=[b_Sb[sbk]])
                    ud = j - 4 * i
                    pt_ = fin_cnt[0] % 2
                    fin_cnt[0] += 1

                    def ex(e):
                        ins = None
                        for u in range(4):
                            usl = slice(u * 128, (u + 1) * 128)
                            jj = 4 * i + u
                            if j == jj:
                                continue
                            v = 0 if j < jj else 1
                            idx = (4 * i - j + 2) if v == 0 else (63 + j - 4 * i - 1)
                            ins = e.activation(Pt[pk][:, usl], Sb[sbk][:, usl], AF.Exp,
                                               bias=btab[:, hm, idx:idx + 1], scale=1.0)
                        return ins
                    fw.op(act, ex, reads=[b_Sb[sbk], b_btab], writes=[b_Pt[pk]])
                    dsl = slice(ud * 128, (ud + 1) * 128)
                    fw.op(act, lambda e: e.activation(Ptmp[pt_][:, :], Sb[sbk][:, dsl], AF.Exp,
                                                      bias=negC[:, m * 2 + hl:m * 2 + hl + 1], scale=1.0),
                          reads=[b_Sb[sbk], b_negC], writes=[b_Ptmp[pt_]])
                    fw.op(dve, lambda e: e.tensor_tensor(Pt[pk][:, dsl], Ptmp[pt_][:, :], Etab[:, hl, :], ALU.mult),
                          reads=[b_Ptmp[pt_], b_E], writes=[b_Pt[pk]])

            def pv(k):
                i, hl, j, m = steps[k]
                blk = i * 2 + hl
                os_ = blk % 2
                pk = k % 4

                def mm(e):
                    ins = None
                    if j == 0 and m == 0:
                        for bk in range(3):
                            e.matmul(Ob[os_][bk][:, :], zl[:, :], zr[:, :], start=True, stop=False)
                    for u in range(4):
                        a = m * 4 + u
                        off = (a % 3) * 130
                        ins = e.matmul(Ob[os_][a // 3][:, off:off + 129], Pt[pk][:, u * 128:(u + 1) * 128],
                                       Vp[:, j, hl, 0:129], start=False, stop=(j == 63))
                    return ins
                fw.op(pe, mm, reads=[b_Pt[pk], b_Vones, b_zz] + [b_V[(j // 8) * 2 + hl]], writes=b_Ob[os_])
                if j == 63 and m == 1:
                    finalize(i, hl, os_)

            def finalize(i, hl, os_):
                sl = i % 2
                yk = (i * 2 + hl) % 2
                for u in range(4):
                    f = u % 2
                    a1, a2 = u, 4 + u
                    O1 = Ob[os_][a1 // 3][:, (a1 % 3) * 130:(a1 % 3) * 130 + 130]
                    O2 = Ob[os_][a2 // 3][:, (a2 % 3) * 130:(a2 % 3) * 130 + 130]
                    obufs = b_Ob[os_]
                    fw.op(dve, lambda e: e.reciprocal(rr[f][:, 0:1], O1[:, 128:129]), reads=obufs, writes=[b_rr[f]])
                    fw.op(dve, lambda e: e.reciprocal(rr[f][:, 1:2], O2[:, 128:129]), reads=obufs, writes=[b_rr[f]])
                    fw.op(dve, lambda e: e.tensor_tensor(rr[f][:, 2:3], rr[f][:, 1:2], neglam[:, :], ALU.mult),
                          reads=[b_rr[f], b_neglam], writes=[b_rr[f]])
                    fw.op(dve, lambda e: e.tensor_scalar_mul(t1[f][:, :], O1[:, 0:128], rr[f][:, 0:1]),
                          reads=obufs + [b_rr[f]], writes=[b_t1[f]])
                    fw.op(dve, lambda e: e.scalar_tensor_tensor(ot[f][:, :], O2[:, 0:128], rr[f][:, 2:3], t1[f][:, :],
                                                                ALU.mult, ALU.add),
                          reads=obufs + [b_rr[f], b_t1[f]], writes=[b_ot[f]])
                    fw.op(dve, lambda e: e.memset(rr[f][:, 3:4], 0.0), writes=[b_rr[f]])
                    fw.op(act, lambda e: e.activation(junk[f][:, :], ot[f][:, :], AF.Square, accum_out=rr[f][:, 3:4]),
                          reads=[b_ot[f]], writes=[b_junk[f], b_rr[f]])
                    fw.op(act, lambda e: e.activation(rr[f][:, 3:4], rr[f][:, 3:4], AF.Ln, bias=RMS_EPS, scale=1.0 / 128),
                          reads=[b_rr[f]], writes=[b_rr[f]])
                    fw.op(act, lambda e: e.activation(rr[f][:, 3:4], rr[f][:, 3:4], AF.Exp, scale=-0.5),
                          reads=[b_rr[f]], writes=[b_rr[f]])
                    fw.op(dve, lambda e: e.scalar_tensor_tensor(t1[f][:, :], ot[f][:, :], rr[f][:, 3:4], gsc[:, :],
                                                                ALU.mult, ALU.mult),
                          reads=[b_ot[f], b_rr[f], b_gsc], writes=[b_t1[f]])
                    fw.op(dve, lambda e: e.tensor_tensor(yo[yk][:, u, :], t1[f][:, :], gq[i % 3][:, u, hl * 128:(hl + 1) * 128],
                                                         ALU.mult),
                          reads=[b_t1[f], b_gq[i % 3]], writes=[b_yo[yk]])
                fw.dma(pool, yav[i][:, :, hl * 128:(hl + 1) * 128], yo[yk][:, :, :], reads=[b_yo[yk]], writes=[bYA])

            nsteps = len(steps) if attn_steps is None else attn_steps
            for k in range(nsteps + 1):
                if k < nsteps:
                    qk_exp(k)
                if k >= 1:
                    pv(k - 1)
            fw.barrier(release=False)


    U = dscr("U", [S, 256], BF16)
    bU = Buf("U", stream=True)
    if "ssd" in phases:
        convw_d = din("convw", [128, 4, 5])
        convb_d = din("convb", [128, 4])
        ssdp_d = din("ssdp", [1, 20])
        tri_d = din("tri", [128, 4, 128])
        mask4_d = din("mask4", [128, 2, 512], BF16)
        ident2_d = din("ident2", [128, 128], BF16)
        fw.barrier(release=True)
        with ExitStack() as es:
            def sb(name, shape, dt):
                return es.enter_context(nc.sbuf_tensor(name, list(shape), dt))

            def ps(name, shape, dt=F32):
                return es.enter_context(nc.psum_tensor(name, list(shape), dt))

            BcT = sb("BcT", [128, S], BF16)
            CcT = sb("CcT", [128, S], BF16)
            xtm = sb("xtm", [128, 64, 256], BF16)
            Btm = sb("Btm", [128, 64, 128], BF16)
            yacc = sb("yacc", [128, 64, 256], BF16)
            dtr = sb("dtr", [128, 64, 8], F32)
            dtA = sb("dtA", [128, 64, 8], F32)
            tmpa = sb("tmpa", [128, 64, 8], F32)
            tri = sb("tri_sb", [128, 4, 128], F32)
            mask4 = sb("mask4_sb", [128, 2, 512], BF16)
            identb = sb("ident2_sb", [128, 128], BF16)
            convw = sb("convw_sb", [128, 4, 5], F32)
            convb = sb("convb_sb", [128, 4], F32)
            ssdp = sb("ssdp_sb", [128, 20], F32)
            negA = sb("negA", [128, 8], F32)
            H = sb("H", [128, 256], F32)
            Hbf = sb("Hbf", [128, 256], BF16)
            tH = sb("tH", [128, 256], F32)
            R = [sb("R%d" % i, [128, 4, 128], F32) for i in range(2)]
            dec = [sb("dec%d" % i, [128, 4, 128], BF16) for i in range(2)]
            MT = [sb("MT%d" % i, [128, 4, 128], BF16) for i in range(2)]
            cbT = [sb("cbT%d" % i, [128, 128], BF16) for i in range(2)]
            xdt = [sb("xdt%d" % i, [128, 4, 64], BF16) for i in range(2)]
            xdte = [sb("xdte%d" % i, [128, 4, 64], BF16) for i in range(2)]
            eacs = [sb("eacs%d" % i, [128, 8], F32) for i in range(2)]
            tY = [sb("tY%d" % i, [128, 4, 64], F32) for i in range(2)]

            bc = [ps("bc%d" % i, [128, 512]) for i in range(2)]
            cs = ps("cs", [128, 512])
            cbp = ps("cbp", [128, 512])
            Yp = ps("Yp", [128, 512])
            Yoff = ps("Yoff", [128, 512])
            Snew = ps("Snew", [128, 512])
            trp = ps("trp", [128, 1024], BF16)

            (b_BcT, b_CcT, b_dtr, b_dtA, b_tmpa, b_tri, b_mask4, b_identb, b_convw, b_convb, b_ssdp,
             b_negA, b_H, b_Hbf, b_tH, b_cs, b_cbp, b_Yp, b_Yoff, b_Snew, b_trp) = [Buf() for _ in range(21)]
            b_xtm = [Buf() for _ in range(2)]
            b_Btm = Buf()
            b_yacc = [Buf() for _ in range(64)]
            b_R = [Buf() for _ in range(2)]
            b_dec = [Buf() for _ in range(2)]
            b_MT = [Buf() for _ in range(2)]
            b_cbT = [Buf() for _ in range(2)]
            b_xdt = [Buf() for _ in range(2)]
            b_xdte = [Buf() for _ in range(2)]
            b_eacs = [Buf() for _ in range(2)]
            b_tY = [Buf() for _ in range(2)]
            b_bc = [Buf() for _ in range(2)]

            fw.dma(sp, tri[:, :, :], tri_d, writes=[b_tri])
            fw.dma(sp, mask4[:, :, :], mask4_d, writes=[b_mask4])
            fw.dma(sp, identb[:, :], ident2_d, writes=[b_identb])
            fw.dma(sp, convw[:, :, :], convw_d, writes=[b_convw])
            fw.dma(sp, convb[:, :], convb_d, writes=[b_convb])
            fw.dma(sp, ssdp[:, :], ssdp_d.rearrange("a b -> (a b)").partition_broadcast(128), writes=[b_ssdp])
            fw.dma(sp, dtr[:, :, :], DT[:, :].rearrange("(c p) k -> p c k", p=128), reads=[bDT], writes=[b_dtr])

            with ExitStack() as es2:
                xin = es2.enter_context(nc.sbuf_tensor("xin", [128, S + 4], F32))
                acc = [es2.enter_context(nc.sbuf_tensor("cacc%d" % i, [128, 1024], F32)) for i in range(2)]
                xc = es2.enter_context(nc.sbuf_tensor("xc", [128, S], BF16))
                b_xin, b_xpad, b_xc = Buf(), Buf(), Buf()
                b_acc = [Buf() for _ in range(2)]
                fw.op(dve, lambda e: e.memset(xin[:, 0:2], 0.0), writes=[b_xpad])
                fw.op(dve, lambda e: e.memset(xin[:, S + 2:S + 4], 0.0), writes=[b_xpad])
                qc = 0
                for cg in range(4):
                    if cg < 2:
                        src, bsrc, dst, bdst = XT[cg * 128:(cg + 1) * 128, :], bXT, xc, b_xc
                    elif cg == 2:
                        src, bsrc, dst, bdst = BT[:, :], bBT, BcT, b_BcT
                    else:
                        src, bsrc, dst, bdst = CT[:, :], bCT, CcT, b_CcT
                    fw.dma(sp, xin[:, 2:S + 2], src, reads=[bsrc], writes=[b_xin])
                    for q in range(8):
                        a_ = qc % 2
                        qc += 1
                        c0 = q * 1024
                        fw.op(dve, lambda e: e.tensor_scalar_mul(acc[a_][:, :], xin[:, c0:c0 + 1024], convw[:, cg, 0:1]),
                              reads=[b_xin, b_xpad, b_convw], writes=[b_acc[a_]])
                        for k in range(1, 5):
                            fw.op(dve, lambda e: e.scalar_tensor_tensor(acc[a_][:, :], xin[:, c0 + k:c0 + k + 1024],
                                                                        convw[:, cg, k:k + 1], acc[a_][:, :],
                                                                        ALU.mult, ALU.add),
                                  reads=[b_xin, b_xpad, b_convw, b_acc[a_]], writes=[b_acc[a_]])
                        fw.op(act, lambda e: e.activation(dst[:, c0:c0 + 1024], acc[a_][:, :], AF.Silu,
                                                          bias=convb[:, cg:cg + 1], scale=1.0),
                              reads=[b_acc[a_], b_convb], writes=[bdst])
                    if cg < 3:
                        for cc in range(0, 64, 8):
                            def tr(e):
                                ins = None
                                for i_ in range(8):
                                    ins = e.transpose(trp[:, i_ * 128:(i_ + 1) * 128],
                                                      dst[:, (cc + i_) * 128:(cc + i_ + 1) * 128], identb[:, :])
                                return ins
                            fw.op(pe, tr, reads=[bdst, b_identb], writes=[b_trp])
                            if cg < 2:
                                fw.op(dve, lambda e: e.tensor_copy(xtm[:, cc:cc + 8, cg * 128:(cg + 1) * 128],
                                                                   trp[:, :].rearrange("p (a b) -> p a b", a=8)),
                                      reads=[b_trp], writes=[b_xtm[cg]])
                            else:
                                fw.op(dve, lambda e: e.tensor_copy(Btm[:, cc:cc + 8, :],
                                                                   trp[:, :].rearrange("p (a b) -> p a b", a=8)),
                                      reads=[b_trp], writes=[b_Btm])
                fw.barrier(release=False)

            fw.op(act, lambda e: e.activation(negA[:, :], ssdp[:, 0:8], AF.Exp), reads=[b_ssdp], writes=[b_negA])
            fw.op(dve, lambda e: e.tensor_scalar_mul(negA[:, :], negA[:, :], -1.0), reads=[b_negA], writes=[b_negA])
            fw.op(dve, lambda e: e.tensor_tensor(dtr[:, :, :], dtr[:, :, :],
                                                 ssdp[:, 8:16].unsqueeze(1).to_broadcast([128, 64, 8]), ALU.add),
                  reads=[b_dtr, b_ssdp], writes=[b_dtr])
            fw.op(dve, lambda e: e.tensor_scalar_mul(tmpa[:, :, :], dtr[:, :, :], -1.0),
                  reads=[b_dtr], writes=[b_tmpa])
            fw.op(dve, lambda e: e.tensor_tensor(tmpa[:, :, :], tmpa[:, :, :], dtr[:, :, :], ALU.max),
                  reads=[b_dtr, b_tmpa], writes=[b_tmpa])
            fw.op(act, lambda e: e.activation(tmpa[:, :, :], tmpa[:, :, :], AF.Exp, scale=-1.0),
                  reads=[b_tmpa], writes=[b_tmpa])
            fw.op(act, lambda e: e.activation(tmpa[:, :, :], tmpa[:, :, :], AF.Ln, bias=1.0, scale=1.0),
                  reads=[b_tmpa], writes=[b_tmpa])
            fw.op(dve, lambda e: e.tensor_scalar_max(dtr[:, :, :], dtr[:, :, :], 0.0), reads=[b_dtr], writes=[b_dtr])
            fw.op(dve, lambda e: e.tensor_tensor(dtr[:, :, :], dtr[:, :, :], tmpa[:, :, :], ALU.add),
                  reads=[b_dtr, b_tmpa], writes=[b_dtr])
            fw.op(dve, lambda e: e.tensor_tensor(dtA[:, :, :], dtr[:, :, :],
                                                 negA[:, :].unsqueeze(1).to_broadcast([128, 64, 8]), ALU.mult),
                  reads=[b_dtr, b_negA], writes=[b_dtA])

            it = 0
            for d in range(2):
                hd = slice(d * 4, d * 4 + 4)
                Tri = tri[:, d, :]
                last = 127 if d == 0 else 0
                fw.op(dve, lambda e: e.memset(H[:, :], 0.0), writes=[b_H])
                fw.op(dve, lambda e: e.memset(Hbf[:, :], 0.0), writes=[b_Hbf])
                for c in (range(64) if d == 0 else range(63, -1, -1)):
                    k2 = it % 2
                    it += 1
                    csl = slice(c * 128, (c + 1) * 128)
                    fw.op(pe, lambda e: (e.matmul(cs[:, 0:4], Tri, dtA[:, c, hd], start=True, stop=True),
                                         e.matmul(cs[:, 4:8], tri[:, 2, :], dtA[:, c, hd], start=True, stop=True))[1],
                          reads=[b_dtA, b_tri], writes=[b_cs])
                    fw.op(act, lambda e: e.activation(eacs[k2][:, :], cs[:, 0:8], AF.Exp),
                          reads=[b_cs], writes=[b_eacs[k2]])
                    fw.op(pool, lambda e: e.tensor_tensor(R[k2][:, :, :], Tri.unsqueeze(1).to_broadcast([128, 4, 128]),
                                                          dtA[:, c, hd].unsqueeze(2).to_broadcast([128, 4, 128]), ALU.mult),
                          reads=[b_dtA, b_tri], writes=[b_R[k2]])

                    def mm_bc(e):
                        e.matmul(bc[k2][:, :], tri[:, 2, :], R[k2][:, :, :].rearrange("p a b -> p (a b)"),
                                 start=True, stop=False)
                        for h in range(4):
                            e.matmul(bc[k2][:, h * 128:(h + 1) * 128], R[k2][:, h, :], tri[:, 3, :],
                                     start=False, stop=False)
                        return e.matmul(bc[k2][:, :], identb[:, :], mask4[:, d, :], start=False, stop=True)
                    fw.op(pe, mm_bc, reads=[b_R[k2], b_tri, b_identb, b_mask4], writes=[b_bc[k2]])
                    fw.op(act, lambda e: e.activation(dec[k2][:, :, :].rearrange("p a b -> p (a b)"), bc[k2][:, :], AF.Exp),
                          reads=[b_bc[k2]], writes=[b_dec[k2]])
                    fw.op(pe, lambda e: e.matmul(cbp[:, 0:128], BcT[:, csl], CcT[:, csl], start=True, stop=True),
                          reads=[b_BcT, b_CcT], writes=[b_cbp])
                    fw.op(act, lambda e: e.activation(cbT[k2][:, :], cbp[:, 0:128], AF.Copy),
                          reads=[b_cbp], writes=[b_cbT[k2]])
                    fw.op(dve, lambda e: e.tensor_tensor(MT[k2][:, :, :], dec[k2][:, :, :],
                                                         cbT[k2][:, :].unsqueeze(1).to_broadcast([128, 4, 128]), ALU.mult),
                          reads=[b_dec[k2], b_cbT[k2]], writes=[b_MT[k2]])
                    fw.op(pool, lambda e: e.tensor_tensor(xdt[k2][:, :, :], xtm[:, c, :].rearrange("p (h q) -> p h q", h=4),
                                                          dtr[:, c, hd].unsqueeze(2).to_broadcast([128, 4, 64]), ALU.mult),
                          reads=[b_xtm[0], b_xtm[1], b_dtr], writes=[b_xdt[k2]])

                    def mm_y(e):
                        ins = None
                        for h in range(4):
                            ins = e.matmul(Yp[:, h * 64:(h + 1) * 64], MT[k2][:, h, :], xdt[k2][:, h, :],
                                           start=True, stop=True)
                        return ins
                    fw.op(pe, mm_y, reads=[b_MT[k2], b_xdt[k2]], writes=[b_Yp])
                    fw.op(pe, lambda e: e.matmul(Yoff[:, 0:256], CcT[:, csl], Hbf[:, :], start=True, stop=True),
                          reads=[b_CcT, b_Hbf], writes=[b_Yoff])
                    fw.op(dve, lambda e: e.tensor_tensor(tY[k2][:, :, :], Yoff[:, 0:256].rearrange("p (h q) -> p h q", h=4),
                                                         eacs[k2][:, 0:4].unsqueeze(2).to_broadcast([128, 4, 64]), ALU.mult),
                          reads=[b_Yoff, b_eacs[k2]], writes=[b_tY[k2]])
                    tyf = tY[k2][:, :, :].rearrange("p h q -> p (h q)")
                    if d == 0:
                        fw.op(dve, lambda e: e.tensor_tensor(yacc[:, c, :], tyf, Yp[:, 0:256], ALU.add),
                              reads=[b_tY[k2], b_Yp], writes=[b_yacc[c]])
                    else:
                        fw.op(dve, lambda e: e.tensor_tensor(tyf, tyf, Yp[:, 0:256], ALU.add),
                              reads=[b_tY[k2], b_Yp], writes=[b_tY[k2]])
                        fw.op(dve, lambda e: e.tensor_tensor(yacc[:, c, :], yacc[:, c, :], tyf, ALU.add),
                              reads=[b_tY[k2], b_yacc[c]], writes=[b_yacc[c]])
                    fw.op(pool, lambda e: e.tensor_tensor(xdte[k2][:, :, :], xdt[k2][:, :, :],
                                                          dec[k2][:, :, last:last + 1].to_broadcast([128, 4, 64]), ALU.mult),
                          reads=[b_xdt[k2], b_dec[k2]], writes=[b_xdte[k2]])
                    fw.op(pe, lambda e: e.matmul(Snew[:, 0:256], Btm[:, c, :],
                                                 xdte[k2][:, :, :].rearrange("p h q -> p (h q)"), start=True, stop=True),
                          reads=[b_Btm, b_xdte[k2]], writes=[b_Snew])
                    fw.op(dve, lambda e: e.tensor_tensor(tH[:, :].rearrange("p (h q) -> p h q", h=4),
                                                         H[:, :].rearrange("p (h q) -> p h q", h=4),
                                                         eacs[k2][:, 4:8].unsqueeze(2).to_broadcast([128, 4, 64]), ALU.mult),
                          reads=[b_H, b_eacs[k2]], writes=[b_tH])
                    fw.op(dve, lambda e: e.tensor_tensor(H[:, :], tH[:, :], Snew[:, 0:256], ALU.add),
                          reads=[b_tH, b_Snew], writes=[b_H])
                    fw.op(act, lambda e: e.activation(Hbf[:, :], H[:, :], AF.Copy), reads=[b_H], writes=[b_Hbf])

            with ExitStack() as es3:
                zt = [es3.enter_context(nc.sbuf_tensor("zt%d" % i, [128, 8, 256], BF16)) for i in range(2)]
                t32 = [es3.enter_context(nc.sbuf_tensor("t32_%d" % i, [128, 8, 256], F32)) for i in range(2)]
                ut = [es3.enter_context(nc.sbuf_tensor("ut%d" % i, [128, 8, 256], BF16)) for i in range(2)]
                b_zt = [Buf() for _ in range(2)]
                b_t32 = [Buf() for _ in range(2)]
                b_ut = [Buf() for _ in range(2)]
                tmz = TM[:, :].rearrange("(c p) k -> p c k", p=128)
                uv = U[:, :].rearrange("(c p) k -> p c k", p=128)
                for gi_ in range(8):
                    k2 = gi_ % 2
                    c0 = gi_ * 8
                    fw.dma(sp, zt[k2][:, :, :], tmz[:, c0:c0 + 8, 512:768], reads=[bTM], writes=[b_zt[k2]])
                    fw.op(dve, lambda e: e.tensor_tensor(
                        t32[k2][:, :, :].rearrange("p c (h q) -> p c h q", h=4),
                        xtm[:, c0:c0 + 8, :].rearrange("p c (h q) -> p c h q", h=4),
                        ssdp[:, 16:20].unsqueeze(1).unsqueeze(3).to_broadcast([128, 8, 4, 64]), ALU.mult),
                        reads=[b_xtm[0], b_xtm[1], b_ssdp], writes=[b_t32[k2]])
                    fw.op(dve, lambda e: e.tensor_tensor(t32[k2][:, :, :], t32[k2][:, :, :], yacc[:, c0:c0 + 8, :], ALU.add),
                          reads=[b_t32[k2]] + b_yacc[c0:c0 + 8], writes=[b_t32[k2]])
                    fw.op(dve, lambda e: e.tensor_tensor(ut[k2][:, :, :], t32[k2][:, :, :], zt[k2][:, :, :], ALU.mult),
                          reads=[b_t32[k2], b_zt[k2]], writes=[b_ut[k2]])
                    fw.dma(pool, uv[:, c0:c0 + 8, :], ut[k2][:, :, :], reads=[b_ut[k2]], writes=[bU])
                fw.barrier(release=False)


    if "p5" in phases:
        xown_d = din("xown", [2048, D])
        wout_d = din("wout", [2048, D])
        rk_d = nc.dram_tensor("rk", [1, 1], mybir.dt.int32, kind="ExternalInput").ap()
        vecs_d = din("vecs", [5, D])
        ident3_d = din("ident3", [128, 128], BF16)
        UG = nc.dram_tensor("UG", [4, S, 256], BF16)
        YG = nc.dram_tensor("YG", [4, S, 256], BF16)
        bUG, bYG = Buf("UG", stream=True), Buf("YG", stream=True)
        fw.barrier(release=True)
        for src, bsrc, dst, bdst in ((U, bU, UG, bUG), (YA, bYA, YG, bYG)):
            for q_ in range(4):
                fw._pre(pool, [bsrc], [bdst])
                if bdst.dsem is None:
                    bdst.dsem = fw.alloc_dsem()
                    fw.phase_bufs.append(bdst)
                ins = pool.raw.collective_compute("AllGather", ALU.bypass, replica_groups=[[0, 1, 2, 3], [4, 5, 6, 7]],
                                                  ins=[src[q_ * 2048:(q_ + 1) * 2048, :].opt()], outs=[dst[q_].opt()])
                bdst.dsem.count += 1
                ins.then_inc(bdst.dsem.handle, 1)
                bdst.w = (bdst.dsem.handle, bdst.dsem.count)
        with ExitStack() as es:
            def sb(name, shape, dt):
                return es.enter_context(nc.sbuf_tensor(name, list(shape), dt))

            def ps(name, shape, dt=F32):
                return es.enter_context(nc.psum_tensor(name, list(shape), dt))

            wst = sb("wst", [128, 8, D], F32)
            wo = sb("wo", [128, 16, D], BF16)
            vecs = sb("vecs_sb", [128, 5, D], F32)
            identc = sb("ident3_sb", [128, 128], BF16)
            ug = [sb("ug%d" % i, [128, 4, 256], BF16) for i in range(2)]
            ycat = [sb("ycat%d" % i, [128, 8, 256], BF16) for i in range(2)]
            xo = [sb("xo%d" % i, [128, D], F32) for i in range(2)]
            ycT = [sb("ycT%d" % i, [128, 16, 128], BF16) for i in range(2)]
            hh = [sb("hh%d" % i, [128, D], F32) for i in range(2)]
            rsd = [sb("rsd%d" % i, [128, D], F32) for i in range(2)]
            sq5 = sb("sq5", [128, 512], F32)
            st5 = [sb("st5_%d" % i, [128, 12], F32) for i in range(2)]
            mv5 = [sb("mv5_%d" % i, [128, 2], F32) for i in range(2)]
            rs5 = [sb("rs5_%d" % i, [128, 4], F32) for i in range(2)]
            psY = [ps("psY%d" % i, [128, 1024], BF16) for i in range(2)]
            psM = [ps("psM%d" % i, [128, 512]) for i in range(4)]
            b_wst, b_wo, b_vecs, b_identc, b_sq5 = [Buf() for _ in range(5)]
            b_ugl = [[Buf() for _ in range(4)] for _ in range(2)]
            b_yal = [[Buf() for _ in range(4)] for _ in range(2)]
            bo = Buf("out", stream=True)
            b_yc_s = [Buf() for _ in range(2)]
            b_yc_a = [Buf() for _ in range(2)]
            b_xo = [Buf() for _ in range(2)]
            b_ycT = [Buf() for _ in range(2)]
            b_hh = [Buf() for _ in range(2)]
            b_rsd = [Buf() for _ in range(2)]
            b_st5 = [Buf() for _ in range(2)]
            b_mv5 = [Buf() for _ in range(2)]
            b_rs5 = [Buf() for _ in range(2)]
            b_psY = [Buf() for _ in range(2)]
            b_psM = [Buf() for _ in range(4)]

            fw.dma(sp, identc[:, :], ident3_d, writes=[b_identc])
            b_vl = [Buf() for _ in range(5)]
            for v_ in range(5):
                fw.dma(sp, vecs[:, v_, :], vecs_d[v_:v_ + 1, :].rearrange("a b -> (a b)").partition_broadcast(128),
                       writes=[b_vl[v_]])
            wov = wout_d.rearrange("(k p) n -> p k n", p=128)
            for half in range(2):
                fw.dma(sp, wst[:, :, :], wov[:, half * 8:(half + 1) * 8, :], writes=[b_wst])
                fw.op(pool, lambda e: e.tensor_copy(wo[:, half * 8:(half + 1) * 8, :], wst[:, :, :]),
                      reads=[b_wst], writes=[b_wo])

            with sp.raw.register("rq") as rreg:
                sp.raw.reg_load(rreg, rk_d[0:1, 0:1])
                qv = sp.raw.snap(rreg)
                ugv = UG.ap().rearrange("q (r t) c -> r q t c", r=4)
                ygv = YG.ap().rearrange("q (r t) c -> r q t c", r=4)
                UQ = nc.dram_tensor("UQ", [4, 2048, 256], BF16)
                YQ = nc.dram_tensor("YQ", [4, 2048, 256], BF16)
                bUQ = [Buf() for _ in range(4)]
                bYQ = [Buf() for _ in range(4)]
                for rr_ in range(4 if p5_stage >= 2 else 0):
                    fw.dma(sp, UQ[rr_], ugv[rr_, bass.ds(qv, 1), :, :], reads=[bUG], writes=[bUQ[rr_]])
                for rr_ in range(4 if p5_stage >= 2 else 0):
                    fw.dma(sp, YQ[rr_], ygv[rr_, bass.ds(qv, 1), :, :], reads=[bYG], writes=[bYQ[rr_]])
                for tt in range(16 if p5_stage >= 3 else 0):
                    k = tt % 2
                    tsl = slice(tt * 128, (tt + 1) * 128)
                    ugl = b_ugl[k]
                    yal = b_yal[k]
                    for rr_ in range(4):
                        fw.dma(sp, ug[k][:, rr_, :], UQ[rr_, tsl, :], reads=[bUQ[rr_]], writes=[ugl[rr_]])
                    for rr_ in range(4):
                        fw.dma(sp, ycat[k][:, 4 + rr_, :], YQ[rr_, tsl, :], reads=[bYQ[rr_]], writes=[yal[rr_]])
                    fw.dma(sp, xo[k][:, :], xown_d[tsl, :], writes=[b_xo[k]])

                    for gg in range(2):
                        uview = ug[k][:, 2 * gg:2 * gg + 2, :].rearrange("p a b -> p (a b)")
                        fw.op(dve, lambda e: e.memset(rs5[k][:, gg:gg + 1], 0.0), writes=[b_rs5[k]])
                        fw.op(act, lambda e: e.activation(sq5[:, :], uview, AF.Square, accum_out=rs5[k][:, gg:gg + 1]),
                              reads=ugl, writes=[b_sq5, b_rs5[k]])
                    fw.op(act, lambda e: e.activation(rs5[k][:, 0:2], rs5[k][:, 0:2], AF.Ln, bias=RMS_EPS, scale=1.0 / 512),
                          reads=[b_rs5[k]], writes=[b_rs5[k]])
                    fw.op(act, lambda e: e.activation(rs5[k][:, 0:2], rs5[k][:, 0:2], AF.Exp, scale=-0.5),
                          reads=[b_rs5[k]], writes=[b_rs5[k]])
                    for gg in range(2):
                        uview = ug[k][:, 2 * gg:2 * gg + 2, :].rearrange("p a b -> p (a b)")
                        fw.op(dve, lambda e: e.scalar_tensor_tensor(
                            ycat[k][:, 2 * gg:2 * gg + 2, :].rearrange("p a b -> p (a b)"), uview, rs5[k][:, gg:gg + 1],
                            vecs[:, 2, gg * 512:(gg + 1) * 512], ALU.mult, ALU.mult),
                            reads=ugl + [b_rs5[k], b_vl[2]], writes=[b_yc_s[k]])

                    for hf in range(2):
                        def tr(e):
                            ins = None
                            for i_ in range(8):
                                ck = hf * 8 + i_
                                ins = e.transpose(psY[hf][:, i_ * 128:(i_ + 1) * 128],
                                                  ycat[k][:, ck // 2, (ck % 2) * 128:(ck % 2) * 128 + 128], identc[:, :])
                            return ins
                        fw.op(pe, tr, reads=[b_yc_s[k], b_identc] + yal, writes=[b_psY[hf]])
                        fw.op(act, lambda e: e.activation(ycT[k][:, hf * 8:(hf + 1) * 8, :].rearrange("p a b -> p (a b)"),
                                                          psY[hf][:, :], AF.Copy),
                              reads=[b_psY[hf]], writes=[b_ycT[k]])
                    for half in range(2):
                        pm = (tt * 2 + half) % 4

                        def mmo(e):
                            ins = None
                            for ck in range(16):
                                ins = e.matmul(psM[pm][:, :], ycT[k][:, ck, :], wo[:, ck, half * 512:(half + 1) * 512],
                                               start=(ck == 0), stop=(ck == 15))
                            return ins
                        fw.op(pe, mmo, reads=[b_ycT[k], b_wo], writes=[b_psM[pm]])

                    def lnorm(src, bsrc, dst, bdst, gi_, bi_):
                        fw.op(dve, lambda e: (e.bn_stats(st5[k][:, 0:6], src[:, 0:512]),
                                              e.bn_stats(st5[k][:, 6:12], src[:, 512:1024]))[1],
                              reads=[bsrc], writes=[b_st5[k]])
                        fw.op(dve, lambda e: e.bn_aggr(mv5[k][:, :], st5[k][:, :]), reads=[b_st5[k]], writes=[b_mv5[k]])
                        fw.op(act, lambda e: e.activation(rs5[k][:, 2:3], mv5[k][:, 1:2], AF.Ln, bias=LN_EPS, scale=1.0),
                              reads=[b_mv5[k]], writes=[b_rs5[k]])
                        fw.op(act, lambda e: e.activation(rs5[k][:, 2:3], rs5[k][:, 2:3], AF.Exp, scale=-0.5),
                              reads=[b_rs5[k]], writes=[b_rs5[k]])
                        fw.op(dve, lambda e: e.tensor_scalar(dst[:, :], src[:, :], mv5[k][:, 0:1], rs5[k][:, 2:3],
                                                             ALU.subtract, ALU.mult),
                              reads=[bsrc, b_mv5[k], b_rs5[k]], writes=[bdst])
                        fw.op(dve, lambda e: e.tensor_tensor(dst[:, :], dst[:, :], vecs[:, gi_, :], ALU.mult),
                              reads=[bdst, b_vl[gi_]], writes=[bdst])
                        fw.op(dve, lambda e: e.tensor_tensor(dst[:, :], dst[:, :], vecs[:, bi_, :], ALU.add),
                              reads=[bdst, b_vl[bi_]], writes=[bdst])
                    lnorm(xo[k], b_xo[k], hh[k], b_hh[k], 0, 1)
                    for half in range(2):
                        pm = (tt * 2 + half) % 4
                        hs_ = slice(half * 512, (half + 1) * 512)
                        fw.op(dve, lambda e: e.scalar_tensor_tensor(rsd[k][:, hs_], hh[k][:, hs_], ALPHA, psM[pm][:, :],
                                                                    ALU.mult, ALU.add),
                              reads=[b_hh[k], b_psM[pm]], writes=[b_rsd[k]])
                    lnorm(rsd[k], b_rsd[k], hh[k], b_hh[k], 3, 4)
                    fw.dma(pool, out_d[tsl, :], hh[k][:, :], reads=[b_hh[k]], writes=[bo])
            fw.barrier(release=False)

    if debug:
        pass
    fw.barrier(release=False)
    return nc


def host_prep(inputs):
    x = np.asarray(inputs["x"], np.float32)
    w_in = np.asarray(inputs["w_in"], np.float32)[0]
    in_maps = []
    ident = np.eye(128, dtype=np.float32).astype(ml_dtypes.bfloat16)
    blk2 = np.zeros((128, 2), np.float32)
    blk2[0:64, 0] = 1.0
    blk2[64:128, 1] = 1.0
    oz, ox, oB, oC, odf, odb, oq, ok, ov, og = 0, 1024, 2048, 2304, 2560, 2576, 2592, 3616, 4640, 5664
    lng = np.ascontiguousarray(np.asarray(inputs["ln_emb_g"], np.float32).reshape(8, 128).T)
    lnb = np.ascontiguousarray(np.asarray(inputs["ln_emb_b"], np.float32).reshape(8, 128).T)
    for c in range(8):
        b, r = c // 4, c % 4
        hA, hB = core_heads(r)
        gi = r // 2
        cols_fm = np.concatenate([
            np.arange(oq + hA * 128, oq + hA * 128 + 128), np.arange(oq + hB * 128, oq + hB * 128 + 128),
            np.arange(ok + hA * 128, ok + hA * 128 + 128), np.arange(ok + hB * 128, ok + hB * 128 + 128),
            np.arange(ox + r * 256, ox + r * 256 + 256),
            np.arange(oB + gi * 128, oB + gi * 128 + 128),
            np.arange(oC + gi * 128, oC + gi * 128 + 128)])
        cols_tm = np.concatenate([
            np.arange(ov + hA * 128, ov + hA * 128 + 128), np.arange(ov + hB * 128, ov + hB * 128 + 128),
            np.arange(og + hA * 128, og + hA * 128 + 128), np.arange(og + hB * 128, og + hB * 128 + 128),
            np.arange(oz + r * 256, oz + r * 256 + 256),
            np.arange(odf + r * 4, odf + r * 4 + 4), np.arange(odb + r * 4, odb + r * 4 + 4)])
        m = {
            "xb": np.ascontiguousarray(x[b]),
            "wfm": np.ascontiguousarray(w_in[:, cols_fm]),
            "wtm": np.ascontiguousarray(w_in[:, cols_tm]),
            "lneg": lng, "lneb": lnb, "ident": ident, "blk2": blk2,
        }
        slopes = [2.0 ** -(hh + 1) for hh in (hA, hB)]
        posr = np.zeros((2, 2, 2, 512), np.float32)
        tabc = np.zeros((128, 2, 126), np.float32)
        Et = np.zeros((128, 2, 128), np.float32)
        tt = np.arange(512)
        pp = np.arange(128, dtype=np.float64)
        for hl, ms in enumerate(slopes):
            A = ms * 128.0 * (tt // 128)
            Bv = ms * (tt % 128)
            posr[hl, 0, 0], posr[hl, 0, 1] = -A, -Bv
            posr[hl, 1, 0], posr[hl, 1, 1] = A, Bv
            for o in range(-2, 61):
                tabc[:, hl, o + 2] = ms * pp - ms * 128.0 * o
            for o in range(1, 64):
                tabc[:, hl, 63 + o - 1] = -ms * pp - ms * 128.0 * o
            Et[:, hl, :] = np.exp(-ms * np.abs(pp[:, None] - pp[None, :]))
        m["posr"] = posr.astype(ml_dtypes.bfloat16)
        m["tab"] = tabc
        m["Etab"] = Et
        m["lamv"] = np.stack([np.asarray(inputs[k_], np.float32)[0] for k_ in
                              ("lambda_q1", "lambda_k1", "lambda_q2", "lambda_k2")])
        m["subg"] = np.asarray(inputs["subln_g"], np.float32).reshape(1, 128)
        cw = np.asarray(inputs["conv_w"], np.float32)[0]
        cb = np.asarray(inputs["conv_b"], np.float32)[0]
        chans = [np.arange(r * 256, r * 256 + 128), np.arange(r * 256 + 128, r * 256 + 256),
                 np.arange(1024 + gi * 128, 1024 + gi * 128 + 128), np.arange(1280 + gi * 128, 1280 + gi * 128 + 128)]
        m["convw"] = np.ascontiguousarray(np.stack([cw[:, ch].T for ch in chans], axis=1))
        m["convb"] = np.ascontiguousarray(np.stack([cb[ch] for ch in chans], axis=1))
        hs = slice(r * 4, r * 4 + 4)
        m["ssdp"] = np.concatenate([np.asarray(inputs[k_], np.float32)[0][hs] for k_ in
                                    ("A_log_fwd", "A_log_bwd", "dt_bias_fwd", "dt_bias_bwd", "D_skip")]).reshape(1, 20)
        kk = np.arange(128)
        triU = (kk[:, None] <= kk[None, :]).astype(np.float32)
        triL = (kk[:, None] >= kk[None, :]).astype(np.float32)
        m["tri"] = np.ascontiguousarray(np.stack([triU, triL, np.ones((128, 128), np.float32),
                                                  -np.ones((128, 128), np.float32)], axis=1))
        mF = np.where(kk[None, :] < kk[:, None], -30000.0, 0.0).astype(np.float32)
        mB = np.where(kk[None, :] > kk[:, None], -30000.0, 0.0).astype(np.float32)
        m["mask4"] = np.ascontiguousarray(np.stack([np.tile(mF, (1, 4)), np.tile(mB, (1, 4))], axis=1)).astype(ml_dtypes.bfloat16)
        m["ident2"] = ident
        m["ident3"] = ident
        m["xown"] = np.ascontiguousarray(x[b, r * 2048:(r + 1) * 2048])
        m["rk"] = np.array([[r]], np.int32)
        w_out = np.asarray(inputs["w_out"], np.float32)[0]
        rows = [np.arange(0, 1024)]
        for rr in range(4):
            for hh_ in core_heads(rr):
                rows.append(np.arange(1024 + hh_ * 128, 1024 + hh_ * 128 + 128))
        m["wout"] = np.ascontiguousarray(w_out[np.concatenate(rows)])
        m["vecs"] = np.stack([np.asarray(inputs["ln_emb_g"], np.float32), np.asarray(inputs["ln_emb_b"], np.float32),
                              np.asarray(inputs["ssm_norm_g"], np.float32)[0], np.asarray(inputs["ln_g"], np.float32)[0],
                              np.asarray(inputs["ln_b"], np.float32)[0]])
        in_maps.append(m)
    return in_maps


def kernel(**inputs):
    in_maps = host_prep(inputs)
    nc = build_program()
    res = run_bass_kernel_spmd(nc, in_maps, core_ids=list(range(8)))
    out = np.zeros((2, S, D), np.float32)
    for c in range(8):
        b, r = c // 4, c % 4
        out[b, r * 2048:(r + 1) * 2048] = res.results[c]["out"]
    return out
```
